# Optimizing a Trainium2 kernel written in Bass

```python
import jax, jax.numpy as jnp
from jax import lax
import numpy as np

D_MODEL = 2048
BATCH = 4
SEQ = 4096
DEPTH = 1

D_MIX = D_MODEL
RET_HEAD_DIM = 128
RET_HEADS = (D_MIX // 2) // RET_HEAD_DIM
RET_WIDTH = RET_HEADS * RET_HEAD_DIM
RET_CHUNK = 128
RET_THETA = 10000.0
RET_DECAY_BASE = 5
GN_EPS = 1e-5
ATT_HEAD_DIM = 128
ATT_HEADS = (D_MIX - RET_WIDTH) // ATT_HEAD_DIM
ATT_KV_HEADS = 2
ATT_WIDTH = ATT_HEADS * ATT_HEAD_DIM
ATT_KV_WIDTH = ATT_KV_HEADS * ATT_HEAD_DIM
WINDOW = 128
ATT_BLOCK = 128
ROPE_THETA = 500000.0
ROPE_DIMS = ATT_HEAD_DIM // 4
NORM_EPS = 1e-6
MASK_VALUE = -1e30
D_IN_PROJ = 4 * RET_WIDTH + 2 * ATT_WIDTH + 2 * ATT_KV_WIDTH

kernel_name = "hybrid_retention_window_gqa_block"


def rms_norm(x, w):
    xf = x.astype(jnp.float32)
    y = xf * lax.rsqrt(jnp.mean(xf * xf, axis=-1, keepdims=True) + NORM_EPS)
    return (y * w.astype(jnp.float32)).astype(x.dtype)


def rotary(x, rot_dims, theta):
    S = x.shape[1]
    half = rot_dims // 2
    inv_freq = theta ** (-jnp.arange(half, dtype=jnp.float32) / half)
    ang = jnp.arange(S, dtype=jnp.float32)[:, None] * inv_freq[None, :]
    cos = jnp.cos(ang)[None, :, None, :]
    sin = jnp.sin(ang)[None, :, None, :]
    xr = x[..., :rot_dims].astype(jnp.float32)
    x1, x2 = xr[..., :half], xr[..., half:]
    rot = jnp.concatenate([x1 * cos - x2 * sin, x2 * cos + x1 * sin], axis=-1).astype(x.dtype)
    return jnp.concatenate([rot, x[..., rot_dims:]], axis=-1)


def retention_chunkwise(q, k, v, log_gamma, include_diag):
    B, H, S, Dk = q.shape
    Dv = v.shape[-1]
    C = RET_CHUNK
    NC = S // C
    qc = q.reshape(B, H, NC, C, Dk)
    kc = k.reshape(B, H, NC, C, Dk)
    vc = v.reshape(B, H, NC, C, Dv)
    idx = jnp.arange(C, dtype=jnp.float32)
    diff = idx[:, None] - idx[None, :]
    mask = (diff >= 0) if include_diag else (diff > 0)
    decay = jnp.where(mask[None], jnp.exp(log_gamma[:, None, None] * jnp.where(mask, diff, 0.0)[None]), 0.0)
    scores = jnp.einsum('bhnik,bhnjk->bhnij', qc, kc) * decay[None, :, None]
    inner = jnp.einsum('bhnij,bhnjv->bhniv', scores, vc)
    k_dec = jnp.exp(log_gamma[:, None] * (C - 1 - idx)[None, :])
    kv_chunk = jnp.einsum('bhnjk,hj,bhnjv->nbhkv', kc, k_dec, vc)
    chunk_decay = jnp.exp(log_gamma * C)[None, :, None, None]

    def step(state, kv):
        return state * chunk_decay + kv, state

    _, prev = lax.scan(step, jnp.zeros((B, H, Dk, Dv), jnp.float32), kv_chunk)
    q_dec = jnp.exp(log_gamma[:, None] * (idx + 1.0)[None, :])
    cross = jnp.einsum('bhnik,hi,nbhkv->bhniv', qc, q_dec, prev)
    return (inner + cross).reshape(B, H, S, Dv)


def retention_branch(rq, rk, rv, decay_fwd, decay_bwd, gn_w, gn_b):
    B, S, _ = rq.shape
    q = rotary(rq.reshape(B, S, RET_HEADS, RET_HEAD_DIM), RET_HEAD_DIM, RET_THETA)
    k = rotary(rk.reshape(B, S, RET_HEADS, RET_HEAD_DIM), RET_HEAD_DIM, RET_THETA)
    q = q.astype(jnp.float32).transpose(0, 2, 1, 3)
    k = k.astype(jnp.float32).transpose(0, 2, 1, 3) * (RET_HEAD_DIM ** -0.5)
    v = rv.reshape(B, S, RET_HEADS, RET_HEAD_DIM).astype(jnp.float32).transpose(0, 2, 1, 3)
    lg_f = jax.nn.log_sigmoid(decay_fwd.astype(jnp.float32))
    lg_b = jax.nn.log_sigmoid(decay_bwd.astype(jnp.float32))
    y_f = retention_chunkwise(q, k, v, lg_f, True)
    y_b = retention_chunkwise(q[:, :, ::-1], k[:, :, ::-1], v[:, :, ::-1], lg_b, False)[:, :, ::-1]
    y = (y_f + y_b).transpose(0, 2, 1, 3)
    mu = jnp.mean(y, axis=-1, keepdims=True)
    var = jnp.mean(jnp.square(y - mu), axis=-1, keepdims=True)
    y = ((y - mu) * lax.rsqrt(var + GN_EPS)).reshape(B, S, RET_WIDTH)
    return y * gn_w.astype(jnp.float32) + gn_b.astype(jnp.float32)


def window_attention_branch(aq, ak, av, sink):
    B, S, _ = aq.shape
    T = ATT_BLOCK
    NB = S // T
    G = ATT_HEADS // ATT_KV_HEADS
    q = rotary(aq.reshape(B, S, ATT_HEADS, ATT_HEAD_DIM), ROPE_DIMS, ROPE_THETA)
    k = rotary(ak.reshape(B, S, ATT_KV_HEADS, ATT_HEAD_DIM), ROPE_DIMS, ROPE_THETA)
    v = av.reshape(B, S, ATT_KV_HEADS, ATT_HEAD_DIM)
    qb = q.reshape(B, NB, T, ATT_KV_HEADS, G, ATT_HEAD_DIM)
    pad = ((0, 0), (T, T), (0, 0), (0, 0))
    kp = jnp.pad(k, pad).reshape(B, NB + 2, T, ATT_KV_HEADS, ATT_HEAD_DIM)
    vp = jnp.pad(v, pad).reshape(B, NB + 2, T, ATT_KV_HEADS, ATT_HEAD_DIM)
    kw = jnp.concatenate([kp[:, :-2], kp[:, 1:-1], kp[:, 2:]], axis=2)
    vw = jnp.concatenate([vp[:, :-2], vp[:, 1:-1], vp[:, 2:]], axis=2)
    qpos = jnp.arange(S).reshape(NB, T)
    kpos_p = (jnp.arange(S + 2 * T) - T).reshape(NB + 2, T)
    kpos = jnp.concatenate([kpos_p[:-2], kpos_p[1:-1], kpos_p[2:]], axis=1)
    valid = (jnp.abs(qpos[:, :, None] - kpos[:, None, :]) <= WINDOW) & (kpos[:, None, :] >= 0) & (kpos[:, None, :] < S)
    s = jnp.einsum('bntkgd,bnskd->bnkgts', qb.astype(jnp.float32), kw.astype(jnp.float32)) * (ATT_HEAD_DIM ** -0.5)
    s = jnp.where(valid[None, :, None, None], s, MASK_VALUE)
    sink_l = sink.astype(jnp.float32).reshape(ATT_KV_HEADS, G)[None, None, :, :, None, None]
    m = jnp.maximum(jnp.max(s, axis=-1, keepdims=True), sink_l)
    p = jnp.exp(s - m)
    denom = jnp.sum(p, axis=-1, keepdims=True) + jnp.exp(sink_l - m)
    out = jnp.einsum('bnkgts,bnskd->bntkgd', p / denom, vw.astype(jnp.float32))
    return out.reshape(B, S, ATT_WIDTH)


def hybrid_layer(x, w_in, w_out, norm_w, decay_fwd, decay_bwd, gn_w, gn_b, sink):
    h = rms_norm(x, norm_w)
    proj = h @ w_in
    splits = [RET_WIDTH, 2 * RET_WIDTH, 3 * RET_WIDTH, 4 * RET_WIDTH,
              4 * RET_WIDTH + ATT_WIDTH, 4 * RET_WIDTH + ATT_WIDTH + ATT_KV_WIDTH,
              4 * RET_WIDTH + ATT_WIDTH + 2 * ATT_KV_WIDTH]
    rq, rk, rv, rg, aq, ak, av, ag = jnp.split(proj, splits, axis=-1)
    y_ret = retention_branch(rq, rk, rv, decay_fwd, decay_bwd, gn_w, gn_b) * jax.nn.silu(rg.astype(jnp.float32))
    y_att = window_attention_branch(aq, ak, av, sink) * jax.nn.silu(ag.astype(jnp.float32))
    y = jnp.concatenate([y_ret, y_att], axis=-1).astype(x.dtype)
    return x + y @ w_out


def setup_inputs(seed: int = 0) -> dict:
    key = jax.random.key(seed)
    ks = jax.random.split(key, 10)
    x = jax.random.normal(ks[0], (BATCH, SEQ, D_MODEL), jnp.float32)
    w_in = jax.random.normal(ks[1], (DEPTH, D_MODEL, D_IN_PROJ), jnp.float32) * (D_MODEL ** -0.5)
    w_out = jax.random.normal(ks[2], (DEPTH, D_MIX, D_MODEL), jnp.float32) * (D_MIX ** -0.5)
    norm_w = 1.0 + 0.01 * jax.random.normal(ks[3], (DEPTH, D_MODEL), jnp.float32)
    exps = RET_DECAY_BASE + jnp.arange(RET_HEADS, dtype=jnp.float32)
    base_logit = jnp.log(2.0 ** exps - 1.0)
    decay_fwd = base_logit[None] + 0.01 * jax.random.normal(ks[4], (DEPTH, RET_HEADS), jnp.float32)
    decay_bwd = base_logit[None] + 0.01 * jax.random.normal(ks[5], (DEPTH, RET_HEADS), jnp.float32)
    gn_w = 1.0 + 0.01 * jax.random.normal(ks[6], (DEPTH, RET_WIDTH), jnp.float32)
    gn_b = 0.01 * jax.random.normal(ks[7], (DEPTH, RET_WIDTH), jnp.float32)
    sink = jax.random.normal(ks[8], (DEPTH, ATT_HEADS), jnp.float32)
    final_norm_w = 1.0 + 0.01 * jax.random.normal(ks[9], (D_MODEL,), jnp.float32)
    return {"x": x, "w_in": w_in, "w_out": w_out, "norm_w": norm_w,
            "decay_fwd": decay_fwd, "decay_bwd": decay_bwd, "gn_w": gn_w, "gn_b": gn_b,
            "sink": sink, "final_norm_w": final_norm_w}


def reference(x, w_in, w_out, norm_w, decay_fwd, decay_bwd, gn_w, gn_b, sink, final_norm_w):
    for l in range(DEPTH):
        x = hybrid_layer(x, w_in[l], w_out[l], norm_w[l], decay_fwd[l], decay_bwd[l], gn_w[l], gn_b[l], sink[l])
    return rms_norm(x, final_norm_w)
```

```python
import contextlib
import math
import numpy as np
import concourse.bass as bass
import concourse.mybir as mybir
from concourse.bass_utils import run_bass_kernel_spmd

F32 = mybir.dt.float32
BF16 = mybir.dt.bfloat16
AF = mybir.ActivationFunctionType
ALU = mybir.AluOpType
AX = mybir.AxisListType

ENGS = ("pe", "act", "dve", "pool", "sp")
SEQ = 4096
HALF = 2048
DM = 2048
C = 128
NCH = 16
NBLK = 21
LNS = math.log(128.0 ** -0.5)
QS = 128.0 ** -0.5


class View:
    __slots__ = ("ap", "key", "bank")

    def __init__(self, ap, key, bank=None):
        self.ap = ap
        self.key = key
        self.bank = bank


class Buf:
    def __init__(self, name, handle, bank=None):
        self.name = name
        self.h = handle
        self.bank = bank

    def __getitem__(self, idx):
        return View(self.h[idx], (self.name,), self.bank)

    def k(self, sub):
        b = self

        class _K:
            def __getitem__(self, idx):
                return View(b.h[idx], (b.name, sub), b.bank)

        return _K()


def D(ap):
    return View(ap, None)


class Op:
    __slots__ = ("eng", "fn", "reads", "writes", "deps", "stream", "inc", "token", "nm")


class Sched:
    def __init__(self, nc):
        self.nc = nc
        self.ops = {e: [] for e in ENGS}
        self.last_write = {}
        self.readers = {}
        self.all_ops = []
        self.dmas_since = []

    def add(self, eng, fn, reads=(), writes=(), stream=None, nm=""):
        op = Op()
        op.eng = eng
        op.fn = fn
        op.nm = nm
        op.reads = [v.key for v in reads if v.key is not None]
        op.writes = [v.key for v in writes if v.key is not None]
        for v in list(reads) + list(writes):
            if v.bank is not None and ("bank", v.bank) not in op.writes:
                op.writes.append(("bank", v.bank))
        op.stream = stream
        op.inc = False
        op.token = None
        deps = []
        raw = set()
        for r in op.reads:
            w = self.last_write.get(r)
            if w is not None:
                deps.append(w)
                raw.add(id(w))
        for wk in op.writes:
            w = self.last_write.get(wk)
            if w is not None:
                deps.append(w)
            for rd in self.readers.get(wk, ()):
                deps.append(rd)
        fd = []
        seen = set()
        for d in deps:
            if id(d) in seen:
                continue
            seen.add(id(d))
            if d.stream is None and stream is None and d.eng == eng and id(d) not in raw:
                continue
            fd.append(d)
        op.deps = fd
        for wk in op.writes:
            self.last_write[wk] = op
            self.readers[wk] = []
        for r in op.reads:
            self.readers.setdefault(r, []).append(op)
        self.all_ops.append(op)
        self.ops[eng].append(op)
        if stream is not None:
            self.dmas_since.append(op)
        return op

    def barrier(self):
        lasts = [self.ops[e][-1] for e in ENGS if self.ops[e]] + list(self.dmas_since)
        for e in ENGS:
            op = self.add(e, lambda eng: eng.nop(), nm="barrier")
            op.deps = [d for d in lasts if not (d.stream is None and d.eng == e)]
        self.last_write.clear()
        self.readers.clear()
        self.dmas_since = []

    def emit(self, final_wait_ops=()):
        nc = self.nc
        fin = Op()
        fin.eng = "sp"; fin.fn = None; fin.reads = []; fin.writes = []; fin.stream = None
        fin.inc = False; fin.token = None; fin.deps = list(final_wait_ops); fin.nm = "final"
        self.ops["sp"].append(fin)
        self.all_ops.append(fin)
        for op in self.all_ops:
            for d in op.deps:
                d.inc = True
        with contextlib.ExitStack() as es:
            esem = {e: es.enter_context(nc.semaphore("s_" + e)) for e in ENGS}
            ssem = {}
            for op in self.all_ops:
                if op.stream is not None and op.stream not in ssem:
                    ssem[op.stream] = es.enter_context(nc.semaphore("d_" + str(len(ssem))))
            ecount = {e: 0 for e in ENGS}
            scount = {s: 0 for s in ssem}
            for op in self.all_ops:
                if op.stream is not None:
                    scount[op.stream] += 16
                    op.token = (("s", op.stream), scount[op.stream])
                elif op.inc:
                    ecount[op.eng] += 1
                    op.token = (("e", op.eng), ecount[op.eng])
            self.max_counts = (dict(ecount), len(ssem))
            block = es.enter_context(nc.Block())
            engobj = {"pe": "tensor", "act": "scalar", "dve": "vector", "pool": "gpsimd", "sp": "sync"}
            nwaits = {e: 0 for e in ENGS}

            def make(e):
                def body(eng):
                    waited = {}
                    for op in self.ops[e]:
                        need = {}
                        for d in op.deps:
                            sk, val = d.token
                            if waited.get(sk, 0) >= val:
                                continue
                            if need.get(sk, 0) < val:
                                need[sk] = val
                        for sk, val in need.items():
                            sem = esem[sk[1]] if sk[0] == "e" else ssem[sk[1]]
                            eng.wait_ge(sem, val)
                            waited[sk] = val
                            nwaits[e] += 1
                        if op.fn is None:
                            continue
                        inst = op.fn(eng)
                        if op.stream is not None:
                            inst.then_inc(ssem[op.stream], 16)
                        elif op.inc:
                            inst.then_inc(esem[e], 1)
                return body

            for e in ENGS:
                getattr(block, engobj[e])(make(e))
            self.nwaits = nwaits

    def dma(self, eng, out, in_, stream):
        return self.add(eng, lambda e: e.dma_start(out=out.ap, in_=in_.ap), [in_], [out], stream=stream)

    def mm(self, out, lhsT, rhs, start, stop):
        return self.add("pe", lambda e: e.matmul(out.ap, lhsT.ap, rhs.ap, start=start, stop=stop),
                        [lhsT, rhs], [out])

    def tr(self, out, in_, ident):
        return self.add("pe", lambda e: e.transpose(out.ap, in_.ap, ident.ap), [in_, ident], [out])

    def act(self, out, in_, func, bias=None, scale=None, accum_out=None):
        reads = [in_]
        kw = {}
        if bias is not None:
            if isinstance(bias, View):
                reads.append(bias); kw["bias"] = bias.ap
            else:
                kw["bias"] = bias
        if scale is not None:
            if isinstance(scale, View):
                reads.append(scale); kw["scale"] = scale.ap
            else:
                kw["scale"] = scale
        writes = [out]
        if accum_out is not None:
            writes.append(accum_out); kw["accum_out"] = accum_out.ap
        return self.add("act", lambda e: e.activation(out.ap, in_.ap, func, **kw), reads, writes)

    def tt(self, eng, out, in0, in1, op):
        return self.add(eng, lambda e: e.tensor_tensor(out.ap, in0.ap, in1.ap, op), [in0, in1], [out])

    def ts(self, eng, out, in0, s1, op0, s2=None, op1=None):
        reads = [in0]
        a1 = s1.ap if isinstance(s1, View) else s1
        a2 = s2.ap if isinstance(s2, View) else s2
        if isinstance(s1, View): reads.append(s1)
        if isinstance(s2, View): reads.append(s2)
        if op1 is None:
            return self.add(eng, lambda e: e.tensor_scalar(out.ap, in0.ap, a1, None, op0), reads, [out])
        return self.add(eng, lambda e: e.tensor_scalar(out.ap, in0.ap, a1, a2, op0, op1), reads, [out])

    def stt(self, eng, out, in0, scalar, in1, op0, op1):
        reads = [in0, in1]
        a = scalar.ap if isinstance(scalar, View) else scalar
        if isinstance(scalar, View): reads.append(scalar)
        return self.add(eng, lambda e: e.scalar_tensor_tensor(out.ap, in0.ap, a, in1.ap, op0, op1), reads, [out])

    def copy(self, eng, out, in_):
        if eng == "act":
            return self.add(eng, lambda e: e.activation(out.ap, in_.ap, AF.Identity), [in_], [out])
        return self.add(eng, lambda e: e.tensor_copy(out.ap, in_.ap), [in_], [out])

    def memset(self, eng, out, val):
        return self.add(eng, lambda e: e.memset(out.ap, val), [], [out])

    def reduce(self, eng, out, in_, op):
        return self.add(eng, lambda e: e.tensor_reduce(out.ap, in_.ap, AX.X, op), [in_], [out])

    def recip(self, out, in_):
        return self.add("dve", lambda e: e.reciprocal(out.ap, in_.ap), [in_], [out])


def V2(view, ap):
    return View(ap, view.key, view.bank)


def build_program(stop=99, n0=16):
    LVL = _DEBUG.get('lvl', 99)
    nc = bass.Bass("TRN2", target_bir_lowering=False)
    dt = nc.dram_tensor
    xl = dt("xl", [SEQ, DM], F32, kind="ExternalInput").ap()
    wblk = dt("wblk", [NBLK, DM, 512], F32, kind="ExternalInput").ap()
    rotR = dt("rotR", [32, C, 256], F32, kind="ExternalInput").ap()
    rotA = dt("rotA", [17, C, 128], F32, kind="ExternalInput").ap()
    msk = dt("msk", [C, 256], F32, kind="ExternalInput").ap()
    amask = dt("amask", [C, 384], F32, kind="ExternalInput").ap()
    dec = dt("dec", [1, 16], F32, kind="ExternalInput").ap()
    nwd = dt("nw", [C, 16], F32, kind="ExternalInput").ap()
    fnw = dt("fnw", [1, DM], F32, kind="ExternalInput").ap()
    gnw = dt("gnw", [1, 1024], F32, kind="ExternalInput").ap()
    gnb = dt("gnb", [1, 1024], F32, kind="ExternalInput").ap()
    sinkd = dt("sink", [1, 8], F32, kind="ExternalInput").ap()
    expo = dt("expo", [C, 20], F32, kind="ExternalInput").ap()
    identd = dt("identf", [C, C], F32, kind="ExternalInput").ap()
    outd = dt("out", [HALF, DM], F32, kind="ExternalOutput").ap()
    dbgd = dt("dbg", [C, 16384], F32, kind="ExternalOutput").ap() if stop < 99 else None

    with contextlib.ExitStack() as es:
        S = Sched(nc)

        def finish(items):
            S.barrier()
            off = 0
            ops = []
            for i, (v, n) in enumerate(items):
                ops.append(S.dma("sp", D(dbgd[:, off:off + n] if v.ap.dtype == F32 else dbgd[:, off:off + n].bitcast(BF16)), v, ("dbg", i)))
                off += n
            S.emit(ops)
            build_program.stats = (S.max_counts, S.nwaits, {e: len(S.ops[e]) for e in ENGS})
            return nc

        def sbt(name, shape, dtp):
            return es.enter_context(nc.sbuf_tensor(name, shape, dtp))

        A1 = sbt("A1", [C, 32768], BF16)
        A2 = sbt("A2", [C, 32768], BF16)
        WR = [Buf("wr%d" % i, sbt("wr%d" % i, [C, 16, 512], BF16)) for i in range(2)]
        SCR = sbt("SCR", [C, 8192], F32)
        ident = Buf("ident", sbt("ident", [C, C], BF16))
        MT = Buf("MT", sbt("MT", [C, 2, C], F32))
        AM = Buf("AM", sbt("AM", [C, 384], F32))
        nwT = Buf("nwT", sbt("nwT", [C, 16], F32))
        decs = Buf("decs", sbt("decs", [C, 16], F32))
        lgs = Buf("lgs", sbt("lgs", [C, 16], F32))
        ex = Buf("ex", sbt("ex", [C, 20], F32))
        dec4 = Buf("dec4", sbt("dec4", [C, 8, 4], F32))
        gC = Buf("gC", sbt("gC", [C, 16], F32))
        w0 = Buf("w0", sbt("w0", [C, 16, 8], F32))
        sinkb = Buf("sinkb", sbt("sinkb", [C, 8], F32))
        nsink = Buf("nsink", sbt("nsink", [C, 8], F32))
        neghalf = Buf("neghalf", sbt("neghalf", [C, 4], F32))
        S0 = Buf("S0", sbt("S0", [C, 8, C], F32))
        hTh = Buf("hTh", sbt("hTh", [C, 16, C], BF16))
        gnwb = Buf("gnwb", sbt("gnwb", [C, 2, C], F32))
        PS = [es.enter_context(nc.psum_tensor("ps%d" % i, [C, 512], F32)) for i in range(8)]

        hT = Buf("hT", A1[:, :].rearrange("p (k t) -> p k t", k=16))
        yT = Buf("yT", A2[:, :].rearrange("p (k t) -> p k t", k=16))
        Wo = Buf("Wo", A1[:, :].rearrange("p (k t) -> p k t", k=16))
        W0 = [Buf("W0_%d" % b, A2[:, b * 8192:(b + 1) * 8192].rearrange("p (k c) -> p k c", k=16)) for b in range(4)]

        scr_off = [0]

        def carve(name, shape, dtp):
            n = int(np.prod(shape[1:]))
            n32 = n if dtp == F32 else (n + 1) // 2
            ap = SCR[:, scr_off[0]:scr_off[0] + n32]
            scr_off[0] += n32
            assert scr_off[0] <= 8192, (name, scr_off[0])
            if dtp != F32:
                ap = ap.bitcast(dtp)
            if len(shape) == 3:
                ap = ap.rearrange("p (a b) -> p a b", a=shape[1])
            elif len(shape) == 4:
                ap = ap.rearrange("p (a b c) -> p a b c", a=shape[1], b=shape[2])
            return Buf(name, ap)

        def psv(i, name, dtp=F32, shape=None):
            ap = PS[i][:, :]
            if dtp != F32:
                ap = ap.bitcast(dtp)
            if shape is not None:
                if len(shape) == 3:
                    ap = ap[:, 0:shape[1] * shape[2]].rearrange("p (a b) -> p a b", a=shape[1])
                elif len(shape) == 4:
                    ap = ap[:, 0:shape[1] * shape[2] * shape[3]].rearrange("p (a b c) -> p a b c", a=shape[1], b=shape[2])
                else:
                    ap = ap[:, 0:shape[1]]
            return Buf(name, ap, bank=i)

        def bc_mid(view, n):
            return V2(view, view.ap.unsqueeze(1).broadcast_to([C, n, view.ap.shape[1]]))

        def bc_last(view, n):
            return V2(view, view.ap.unsqueeze(2).broadcast_to([C, view.ap.shape[1], n]))

        identf = carve("identf", [C, C], F32)
        S.dma("sp", identf[:], D(identd), "c0")
        S.dma("sp", MT[:], D(msk.rearrange("p (a b) -> p a b", a=2)), "c1")
        S.dma("sp", AM[:], D(amask), "c2")
        S.dma("sp", nwT[:], D(nwd), "c3")
        S.dma("sp", decs[:], D(dec[0, :].partition_broadcast(C)), "c4")
        S.dma("sp", ex[:], D(expo), "c5")
        S.dma("sp", sinkb[:], D(sinkd[0, :].partition_broadcast(C)), "c6")
        S.copy("dve", ident[:], identf[:])
        S.memset("dve", neghalf[:], -0.5)
        S.memset("pool", S0.k(0)[:, 0:4, :], 0.0)
        S.memset("pool", S0.k(1)[:, 4:8, :], 0.0)
        S.ts("dve", nsink[:], sinkb[:], -1.0, ALU.mult)
        e_t = carve("e_t", [C, 16], F32)
        acc = carve("acc", [C, 16], F32)
        S.act(e_t[:], decs[:], AF.Exp, scale=-1.0)
        S.ts("dve", acc[:], e_t[:], -1.0 / 6, ALU.mult, 1.0 / 5, ALU.add)
        for cf in (1.0 / 4, 1.0 / 3, 1.0 / 2, 1.0):
            S.tt("dve", acc[:], acc[:], e_t[:], ALU.mult)
            S.ts("dve", acc[:], acc[:], -1.0, ALU.mult, cf, ALU.add)
        S.tt("dve", acc[:], acc[:], e_t[:], ALU.mult)
        S.ts("dve", lgs[:], acc[:], -1.0, ALU.mult)
        arg4 = carve("arg4", [C, 8, 4], F32)
        for kind, (d0, sgn, col, addc) in enumerate([(0, 1.0, 0, 0.0), (0, -1.0, 0, LNS), (8, 1.0, 2, 0.0), (8, -1.0, 2, LNS)]):
            tmpc = carve("tmpc%d" % kind, [C, 1], F32)
            S.ts("dve", tmpc[:], ex[:, col:col + 1], sgn, ALU.mult)
            S.ts("dve", arg4[:, :, kind], lgs[:, d0:d0 + 8], tmpc[:, 0:1], ALU.mult, addc, ALU.add)
        S.act(dec4[:], arg4[:], AF.Exp)
        argc = carve("argc", [C, 16], F32)
        S.ts("dve", argc[:], lgs[:], float(C), ALU.mult)
        S.act(gC[:], argc[:], AF.Exp)
        argw = carve("argw", [C, 16, 8], F32)
        for c in range(NCH):
            S.ts("dve", argw[:, c, :], lgs[:, 0:8], ex[:, 4 + c:5 + c], ALU.mult, LNS, ALU.add)
        S.act(w0[:], argw[:], AF.Exp)
        if stop == 0:
            return finish([(V2(lgs[:], lgs.h[:, :]), 16), (V2(dec4[:], dec4.h[:, :, :].rearrange("p a b -> p (a b)")), 32),
                           (V2(gC[:], gC.h[:, :]), 16), (V2(w0[:], w0.h[:, :, :].rearrange("p a b -> p (a b)")), 128)])
        S.barrier()
        scr_off[0] = 0

        xs = carve("xs", [C, DM], F32)
        hb = carve("hb", [C, DM], BF16)
        ssq = carve("ssq", [C, 2], F32)
        rstd = carve("rstd", [C, 2], F32)
        nwb = V2(nwT[:], nwT.h[:, :].unsqueeze(2).broadcast_to([C, 16, C]))

        def xprep(row0, dst, it):
            S.dma("sp", xs[:], D(xl[row0:row0 + C, :]), "xs")
            j = it % 2
            S.memset("pool", ssq[:, j:j + 1], 0.0)
            S.act(hb[:], xs[:], AF.Square, accum_out=ssq[:, j:j + 1])
            S.ts("dve", rstd[:, j:j + 1], ssq[:, j:j + 1], 1.0 / DM, ALU.mult, 1e-6, ALU.add)
            S.tt("pool", rstd[:, j:j + 1], rstd[:, j:j + 1], neghalf[:, 0:1], ALU.pow)
            S.ts("dve", hb[:], xs[:], rstd[:, j:j + 1], ALU.mult)
            pa = psv(0 + 2 * j, "ptr%d" % (2 * j), BF16, [C, 8, C])
            pb = psv(1 + 2 * j, "ptr%d" % (2 * j + 1), BF16, [C, 8, C])
            for k in range(16):
                p = pa if k < 8 else pb
                S.tr(p[:, k % 8, :], hb[:, k * C:(k + 1) * C], ident[:])
            S.tt("dve", V2(dst, dst.ap[:, 0:8, :]), pa[:], V2(nwb, nwb.ap[:, 0:8, :]), ALU.mult)
            S.tt("dve", V2(dst, dst.ap[:, 8:16, :]), pb[:], V2(nwb, nwb.ap[:, 8:16, :]), ALU.mult)

        for b in range(4):
            for q in range(4):
                S.dma("pool", W0[b].k(q)[:, 4 * q:4 * q + 4, :],
                      D(wblk[b, 512 * q:512 * (q + 1), :].rearrange("(k p) c -> p k c", p=C)), ("w0", b, q))
        hTc = [carve("hTc%d" % i, [C, 16, C], BF16) for i in range(2)]
        rtab = [carve("rtab%d" % i, [C, 256], F32) for i in range(2)]
        t1 = [carve("t1_%d" % i, [C, 2, C], F32) for i in range(2)]
        t2 = [carve("t2_%d" % i, [C, 2, C], F32) for i in range(2)]
        rot = [carve("rot%d" % i, [C, 2, C], F32) for i in range(2)]
        khat = [carve("khat%d" % i, [C, 2, C], BF16) for i in range(2)]
        vb0 = [carve("vb0_%d" % i, [C, 2, C], BF16) for i in range(2)]
        pS0 = [psv(6, "pS0a", F32, [C, 4, C]), psv(7, "pS0b", F32, [C, 4, C])]

        def rotary_ps(pp4, tab, o_t1, o_t2, o_rot, na):
            c2 = V2(tab[:], tab.h[:, 0:128].unsqueeze(1).broadcast_to([C, na, 128]))
            s2a = V2(tab[:], tab.h[:, 128:192].unsqueeze(1).broadcast_to([C, na, 64]))
            s2b = V2(tab[:], tab.h[:, 192:256].unsqueeze(1).broadcast_to([C, na, 64]))
            full = V2(pp4, pp4.ap.rearrange("p a h f -> p a (h f)"))
            S.tt("dve", o_t1[:], full, c2, ALU.mult)
            if LVL < 32:
                return
            S.tt("dve", o_t2[:, :, 0:64], V2(pp4, pp4.ap[:, :, 1, :]), s2a, ALU.mult)
            S.tt("dve", o_t2[:, :, 64:128], V2(pp4, pp4.ap[:, :, 0, :]), s2b, ALU.mult)
            if LVL < 33:
                return
            S.tt("pool", o_rot[:], o_t1[:], o_t2[:], ALU.add)

        def p0_dst(c):
            return hTc[c % 2][:] if c < NCH - 1 else hTh[:]

        def p0_prep(c):
            xprep(c * C, p0_dst(c), c)
            S.dma("sp", rtab[c % 2][:], D(rotR[c]), ("rt", c % 2))

        def p0_A(c, b, itn):
            dst = p0_dst(c)
            pp = psv(4 + (itn % 2), "pp%d" % (itn % 2), F32, [C, 512])
            for k in range(16):
                S.mm(pp[:], V2(dst, dst.ap[:, k, :]), W0[b].k(k // 4)[:, k, :], start=(k == 0), stop=(k == 15))

        def p0_B(c, b, itn):
            i2 = itn % 2
            pp = psv(4 + i2, "pp%d" % i2, F32, [C, 512])
            pk = V2(pp[:], pp.h[:, :].rearrange("p (a t h f) -> p a t h f", a=2, t=2, h=2)[:, :, 0, :, :])
            rotary_ps(pk, rtab[c % 2], t1[i2], t2[i2], rot[i2], 2)
            S.tt("pool", khat[i2][:], rot[i2][:], bc_last(w0[:, c, 2 * b:2 * b + 2], C), ALU.mult)
            pv = V2(pp[:], pp.h[:, :].rearrange("p (a t d) -> p a t d", a=2, t=2)[:, :, 1, :])
            S.copy("act", vb0[i2][:], pv)
            for a in range(2):
                h = 2 * b + a
                S.mm(pS0[h // 4][:, h % 4, :], khat[i2][:, a, :], vb0[i2][:, a, :], True, True)
            if b % 2 == 1:
                hb_ = b // 2
                S.tt("dve", S0.k(hb_)[:, 4 * hb_:4 * hb_ + 4, :], pS0[hb_][:], S0.k(hb_)[:, 4 * hb_:4 * hb_ + 4, :], ALU.add)

        c_first = 16 - n0
        p0_prep(c_first)
        p0_A(c_first, 0, 0)
        it = 0
        for c in range(c_first, NCH):
            for b in range(4):
                if b == 1 and c + 1 < NCH:
                    p0_prep(c + 1)
                if b < 3:
                    p0_A(c, b + 1, it + 1)
                elif c + 1 < NCH:
                    p0_A(c + 1, 0, it + 1)
                p0_B(c, b, it)
                it += 1
        if stop == 1:
            return finish([(View(S0.h[:, :, :].rearrange("p a b -> p (a b)"), ("S0", 0)), 1024),
                           (V2(hTh[:], hTh.h[:, :, :].rearrange("p a b -> p (a b)")), 1024)])
        S.barrier()

        for c in range(NCH):
            xprep(HALF + c * C, hT[:, :, c * C:(c + 1) * C], c)
        if stop == 2:
            return finish([(View(A1[:, 0:8192], ("hT",)), 4096)])
        S.barrier()
        scr_off[0] = 0

        def rtemp(setn, t):
            off = 16384 + setn * 8192 + t * 2048
            return A2[:, off:off + 2048].rearrange("p (c d) -> p c d", c=16)

        YP = [Buf("YP%d" % s, rtemp(s, 0)) for s in range(2)]
        QB = [Buf("QB%d" % s, rtemp(s, 1)) for s in range(2)]
        KVB = [Buf("KVB%d" % s, rtemp(s, 2)) for s in range(2)]
        SG = [Buf("SG%d" % s, rtemp(s, 3)) for s in range(2)]
        rtab = [carve("rtabR%d" % i, [C, 256], F32) for i in range(2)]
        t1 = [carve("t1R%d" % i, [C, 2, C], F32) for i in range(2)]
        t2 = [carve("t2R%d" % i, [C, 2, C], F32) for i in range(2)]
        rot = [carve("rotR%d" % i, [C, 2, C], F32) for i in range(2)]
        tl = [carve("tl%d" % i, [C, 4, C], BF16) for i in range(2)]
        tT = [carve("tT%d" % i, [C, 4, C], BF16) for i in range(2)]
        Pm = [carve("Pm%d" % i, [C, 2, C], BF16) for i in range(2)]
        vb = [carve("vb%d" % i, [C, C], BF16) for i in range(2)]
        Uf = carve("Uf", [C, C], F32)
        Ub = carve("Ub", [C, C], F32)
        Sf = [carve("Sf%d" % i, [C, C], BF16) for i in range(2)]
        Sb = [carve("Sb%d" % i, [C, C], BF16) for i in range(2)]
        yv = [carve("yv%d" % i, [C, 4, C], F32) for i in range(2)]
        ysq = carve("ysq", [C, 4, C], F32)
        st = [carve("st%d" % i, [C, 4, 4], F32) for i in range(2)]
        yg = [carve("yg%d" % i, [C, 4, C], BF16) for i in range(2)]

        def wload(blk, slot):
            for q in range(4):
                S.dma("pool", WR[slot].k(q)[:, 4 * q:4 * q + 4, :],
                      D(wblk[blk, 512 * q:512 * (q + 1), :].rearrange("(k p) c -> p k c", p=C)), ("wr", slot, q))

        wload(4, 0)
        itR = 0
        for h in range(8):
            slot = h % 2
            if h + 1 < 8:
                wload(4 + h + 1, 1 - slot)
            st_ = h % 2
            S.dma("sp", gnwb[:, 0, :], D(gnw[0, h * C:(h + 1) * C].partition_broadcast(C)), "gw")
            S.dma("sp", gnwb[:, 1, :], D(gnb[0, h * C:(h + 1) * C].partition_broadcast(C)), "gb")
            def r_A(c, itn):
                i2 = itn % 2
                S.dma("sp", rtab[i2][:], D(rotR[16 + c]), ("rtR", i2))
                pp = psv(i2, "ppR%d" % i2, F32, [C, 512])
                for k in range(16):
                    S.mm(pp[:], hT[:, k, c * C:(c + 1) * C], WR[slot].k(k // 4)[:, k, :], start=(k == 0), stop=(k == 15))

            def r_B(c, itn):
                i2 = itn % 2
                pp = psv(i2, "ppR%d" % i2, F32, [C, 512])
                pqk = V2(pp[:], pp.h[:, 0:256].rearrange("p (a h f) -> p a h f", a=2, h=2))
                rotary_ps(pqk, rtab[i2], t1[i2], t2[i2], rot[i2], 2)
                rot4 = V2(rot[i2][:], rot[i2].h[:, :, :].unsqueeze(1).broadcast_to([C, 2, 2, C]))
                tl4 = V2(tl[i2][:], tl[i2].h[:, :, :].rearrange("p (r a) d -> p r a d", r=2))
                d4 = V2(dec4[:], dec4.h[:, h, :].rearrange("p (r a) -> p r a", r=2).unsqueeze(3).broadcast_to([C, 2, 2, C]))
                S.tt("pool", tl4, rot4, d4, ALU.mult)
                S.copy("act", vb[i2][:], pp[:, 256:384])
                S.act(SG[st_][:, c, :], pp[:, 384:512], AF.Silu)
                ptT = psv(2, "ptT", BF16, [C, 4, C])
                for kk in range(4):
                    S.tr(ptT[:, kk, :], tl[i2][:, kk, :], ident[:])
                pkv = psv(5, "pkv", F32, [C, 2, C])
                S.mm(pkv[:, 0, :], tl[i2][:, 1, :], vb[i2][:], True, True)
                S.mm(pkv[:, 1, :], tl[i2][:, 3, :], vb[i2][:], True, True)
                S.copy("dve", tT[i2][:], ptT[:])
                S.copy("act", QB[st_][:, c, :], ptT[:, 2, :])
                psc = psv(3, "psc", F32, [C, 2, C])
                S.mm(psc[:, 0, :], tT[i2][:, 1, :], tT[i2][:, 0, :], True, True)
                S.mm(psc[:, 1, :], tT[i2][:, 3, :], tT[i2][:, 2, :], True, True)
                S.tt("dve", Pm[i2][:], psc[:], MT[:], ALU.mult)
                if c == 0:
                    S.copy("pool", Sf[0][:], S0.k(h // 4)[:, h, :])
                po = psv(4, "poR", F32, [C, C])
                S.mm(po[:], Pm[i2][:, 0, :], vb[i2][:], True, False)
                S.mm(po[:], Pm[i2][:, 1, :], vb[i2][:], False, False)
                S.mm(po[:], tT[i2][:, 0, :], Sf[c % 2][:], False, True)
                S.copy("act", YP[st_][:, c, :], po[:])
                if c == 0:
                    S.tt("dve", Uf[:], pkv[:, 0, :], S0.k(h // 4)[:, h, :], ALU.add)
                else:
                    S.stt("dve", Uf[:], Uf[:], gC[:, h:h + 1], pkv[:, 0, :], ALU.mult, ALU.add)
                if c < NCH - 1:
                    S.ts("pool", Sf[(c + 1) % 2][:], Uf[:], gC[:, h:h + 1], ALU.mult)
                S.copy("act", KVB[st_][:, c, :], pkv[:, 1, :])

            r_A(0, itR)
            for c in range(NCH):
                if c + 1 < NCH:
                    r_A(c + 1, itR + 1)
                r_B(c, itR)
                itR += 1
            S.memset("pool", Ub[:], 0.0)
            S.memset("pool", Sb[1][:], 0.0)
            for bt in range(3, -1, -1):
                j2 = bt % 2
                pcx = psv(6 + j2, "pcx%d" % j2, F32, [C, 4, C])
                for cc in range(3, -1, -1):
                    c = bt * 4 + cc
                    if c < NCH - 1:
                        S.stt("dve", Ub[:], Ub[:], gC[:, 8 + h:9 + h], KVB[st_][:, c + 1, :], ALU.mult, ALU.add)
                        S.ts("pool", Sb[c % 2][:], Ub[:], gC[:, 8 + h:9 + h], ALU.mult)
                    S.mm(pcx[:, cc, :], QB[st_][:, c, :], Sb[c % 2][:], True, True)
                S.tt("dve", yv[j2][:], pcx[:], YP[st_][:, bt * 4:bt * 4 + 4, :], ALU.add)
                S.reduce("dve", st[j2][:, :, 0], yv[j2][:], ALU.add)
                S.tt("pool", ysq[:], yv[j2][:], yv[j2][:], ALU.mult)
                S.reduce("dve", st[j2][:, :, 1], ysq[:], ALU.add)
                S.ts("dve", st[j2][:, :, 2], st[j2][:, :, 0], 1.0 / C, ALU.mult)
                S.tt("dve", st[j2][:, :, 0], st[j2][:, :, 2], st[j2][:, :, 2], ALU.mult)
                S.ts("dve", st[j2][:, :, 1], st[j2][:, :, 1], 1.0 / C, ALU.mult, 1e-5, ALU.add)
                S.tt("dve", st[j2][:, :, 3], st[j2][:, :, 1], st[j2][:, :, 0], ALU.subtract)
                S.tt("pool", st[j2][:, :, 3], st[j2][:, :, 3], neghalf[:, 0:4], ALU.pow)
                for cc in range(4):
                    S.ts("dve", yv[j2][:, cc, :], yv[j2][:, cc, :], st[j2][:, cc, 2:3], ALU.subtract,
                         st[j2][:, cc, 3:4], ALU.mult)
                S.tt("pool", yv[j2][:], yv[j2][:], bc_mid(gnwb[:, 0, :], 4), ALU.mult)
                S.tt("pool", yv[j2][:], yv[j2][:], bc_mid(gnwb[:, 1, :], 4), ALU.add)
                S.tt("dve", yg[j2][:], yv[j2][:], SG[st_][:, bt * 4:bt * 4 + 4, :], ALU.mult)
                pyt = psv(2, "ptT", BF16, [C, 4, C])
                for cc in range(4):
                    S.tr(pyt[:, cc, :], yg[j2][:, cc, :], ident[:])
                S.copy("act", yT.k(h)[:, h, bt * 512:(bt + 1) * 512],
                       V2(pyt[:], pyt.h[:, :, :].rearrange("p a b -> p (a b)")))
        if stop == 3:
            return finish([(View(A2[:, 0:16384], ("yT",)), 8192)])
        S.barrier()
        scr_off[0] = 0

        kT = carve("kT", [C, 2, 17 * C], BF16)
        Vt = carve("Vt", [C, 17, 2, C], BF16)
        atab = [carve("atab%d" % i, [C, 128], F32) for i in range(2)]
        a1 = [carve("a1_0", [C, 2, 32], F32)] * 2
        a2 = [carve("a2_0", [C, 2, 32], F32)] * 2
        qk = [carve("qk%d" % i, [C, 2, C], BF16) for i in range(2)]
        qT = [carve("qT%d" % i, [C, 2, C], BF16) for i in range(2)]
        sm = [carve("sm0", [C, 2, 384], F32)] * 2
        pbf = [carve("pbf%d" % i, [C, 2, 384], BF16) for i in range(2)]
        pT = [carve("pT%d" % i, [C, 2, 3, C], BF16) for i in range(2)]
        sgA = [carve("sgA%d" % i, [C, 2, C], BF16) for i in range(2)]
        ygA = [carve("ygA%d" % i, [C, 2, C], BF16) for i in range(2)]
        sa = [carve("sa%d" % i, [C, 8], F32) for i in range(2)]

        def rot_small(pp3, tab, cols, i2, dst):
            ca = V2(tab[:], tab.h[:, cols:cols + 32].unsqueeze(1).broadcast_to([C, 2, 32]))
            sa_ = V2(tab[:], tab.h[:, cols + 32:cols + 48].unsqueeze(1).broadcast_to([C, 2, 16]))
            sb_ = V2(tab[:], tab.h[:, cols + 48:cols + 64].unsqueeze(1).broadcast_to([C, 2, 16]))
            S.tt("dve", a1[i2][:], V2(pp3, pp3.ap[:, :, 0:32]), ca, ALU.mult)
            S.tt("dve", a2[i2][:, :, 0:16], V2(pp3, pp3.ap[:, :, 16:32]), sa_, ALU.mult)
            S.tt("dve", a2[i2][:, :, 16:32], V2(pp3, pp3.ap[:, :, 0:16]), sb_, ALU.mult)
            S.tt("pool", dst[:, :, 0:32], a1[i2][:], a2[i2][:], ALU.add)

        wload(12, 0)
        wload(13, 1)
        itA = 0

        def kv_A(cc, itn):
            i2 = itn % 2
            S.dma("sp", atab[i2][:], D(rotA[cc]), ("at", i2))
            pp = psv(i2, "ppA%d" % i2, F32, [C, 512])
            for k in range(16):
                lhs = hTh[:, k, :] if cc == 0 else hT[:, k, (cc - 1) * C:cc * C]
                S.mm(pp[:], lhs, WR[0].k(k // 4)[:, k, :], start=(k == 0), stop=(k == 15))

        def kv_B(cc, itn):
            i2 = itn % 2
            pp = psv(i2, "ppA%d" % i2, F32, [C, 512])
            pk3 = V2(pp[:], pp.h[:, 0:256].rearrange("p (a d) -> p a d", a=2))
            S.copy("act", qk[i2][:, :, 32:128], V2(pk3, pk3.ap[:, :, 32:128]))
            rot_small(pk3, atab[i2], 0, i2, qk[i2])
            pkt = psv(2, "pkt", BF16, [C, 2, C])
            for a in range(2):
                S.tr(pkt[:, a, :], qk[i2][:, a, :], ident[:])
            S.copy("dve", kT[:, :, cc * C:(cc + 1) * C], pkt[:])
            S.copy("act", Vt[:, cc, :, :], V2(pp[:], pp.h[:, 256:512].rearrange("p (a d) -> p a d", a=2)))

        kv_A(0, itA)
        for cc in range(17):
            if cc + 1 < 17:
                kv_A(cc + 1, itA + 1)
            kv_B(cc, itA)
            itA += 1
        def q_A(g, c, itn):
            slot = (g + 1) % 2
            i2 = itn % 2
            S.dma("sp", atab[i2][:], D(rotA[c + 1]), ("at", i2))
            pp = psv(i2, "ppA%d" % i2, F32, [C, 512])
            for k in range(16):
                S.mm(pp[:], hT[:, k, c * C:(c + 1) * C], WR[slot].k(k // 4)[:, k, :], start=(k == 0), stop=(k == 15))

        def q_B(g, c, itn):
            kvh = g // 2
            i2 = itn % 2
            nb = 3 if c < NCH - 1 else 2
            nk = nb * C
            pp = psv(i2, "ppA%d" % i2, F32, [C, 512])
            pq3 = V2(pp[:], pp.h[:, 0:256].rearrange("p (a d) -> p a d", a=2))
            S.act(qk[i2][:, :, 32:128], V2(pq3, pq3.ap[:, :, 32:128]), AF.Identity, scale=QS)
            rot_small(pq3, atab[i2], 64, i2, qk[i2])
            S.act(sgA[i2][:], V2(pp[:], pp.h[:, 256:512].rearrange("p (a d) -> p a d", a=2)), AF.Silu)
            pqt = psv(2, "pkt", BF16, [C, 2, C])
            for a in range(2):
                S.tr(pqt[:, a, :], qk[i2][:, a, :], ident[:])
            S.copy("dve", qT[i2][:], pqt[:])
            pss = [psv(3, "pss0", F32, [C, 384]), psv(4, "pss1", F32, [C, 384])]
            for a in range(2):
                S.mm(pss[a][:, 0:nk], qT[i2][:, a, :], kT[:, kvh, c * C:c * C + nk], True, True)
                S.tt("dve", sm[i2][:, a, 0:nk], pss[a][:, 0:nk], AM[:, 0:nk], ALU.add)
            h0 = 2 * g
            S.reduce("dve", sa[i2][:, 0:2], sm[i2][:, :, 0:nk], ALU.max)
            S.ts("dve", sa[i2][:, 0:2], sa[i2][:, 0:2], -1.0, ALU.mult)
            S.tt("dve", sa[i2][:, 0:2], sa[i2][:, 0:2], nsink[:, h0:h0 + 2], ALU.min)
            S.memset("pool", sa[i2][:, 2:4], 0.0)
            for a in range(2):
                S.act(pbf[i2][:, a, 0:nk], sm[i2][:, a, 0:nk], AF.Exp, bias=sa[i2][:, a:a + 1],
                      accum_out=sa[i2][:, 2 + a:3 + a])
                S.act(sa[i2][:, 4 + a:5 + a], sinkb[:, h0 + a:h0 + a + 1], AF.Exp, bias=sa[i2][:, a:a + 1])
            S.tt("dve", sa[i2][:, 6:8], sa[i2][:, 2:4], sa[i2][:, 4:6], ALU.add)
            S.recip(sa[i2][:, 6:8], sa[i2][:, 6:8])
            ppT = psv(5, "ppT", BF16, [C, 2, 3, C])
            for a in range(2):
                for b in range(nb):
                    S.tr(ppT[:, a, b, :], pbf[i2][:, a, b * C:(b + 1) * C], ident[:])
            S.copy("dve", pT[i2][:, :, 0:nb, :], ppT[:, :, 0:nb, :])
            pov = psv(6, "pov", F32, [C, 2, C])
            for a in range(2):
                for b in range(nb):
                    S.mm(pov[:, a, :], pT[i2][:, a, b, :], Vt[:, c + b, kvh, :], b == 0, b == nb - 1)
            for a in range(2):
                S.stt("dve", ygA[i2][:, a, :], pov[:, a, :], sa[i2][:, 6 + a:7 + a], sgA[i2][:, a, :], ALU.mult, ALU.mult)
            pya = psv(7, "pya", BF16, [C, 2, C])
            for a in range(2):
                S.tr(pya[:, a, :], ygA[i2][:, a, :], ident[:])
            S.copy("act", yT.k(8 + h0)[:, 8 + h0:8 + h0 + 2, c * C:(c + 1) * C], pya[:])

        qiters = [(g, c) for g in range(4) for c in range(NCH)]
        q_A(0, 0, itA)
        for n_, (g, c) in enumerate(qiters):
            if c == 0 and g + 1 < 4:
                wload(13 + g + 1, g % 2)
            if n_ + 1 < len(qiters):
                q_A(qiters[n_ + 1][0], qiters[n_ + 1][1], itA + 1)
            q_B(g, c, itA)
            itA += 1
        if stop == 4:
            return finish([(View(A2[:, 16384:32768], ("yT",)), 8192)])
        S.barrier()
        scr_off[0] = 0

        for d in range(4):
            for q in range(4):
                S.dma("pool", Wo.k((d, q))[:, 4 * q:4 * q + 4, d * 512:(d + 1) * 512],
                      D(wblk[17 + d, 512 * q:512 * (q + 1), :].rearrange("(k p) c -> p k c", p=C)), ("wo", d, q))
        xo = carve("xo", [C, DM], F32)
        res = carve("res", [C, DM], F32)
        ob = carve("ob", [C, DM], F32)
        fnwb = carve("fnwb", [C, DM], F32)
        sso = Buf("sso", sbt("sso", [C, 2], F32))
        rso = Buf("rso", sbt("rso", [C, 2], F32))
        S.dma("sp", fnwb[:], D(fnw[0, :].partition_broadcast(C)), "fnw")
        outs = []
        for c in range(NCH):
            j = c % 2
            S.dma("sp", xo[:], D(xl[HALF + c * C:HALF + (c + 1) * C, :]), "xo")
            for d in range(4):
                po = psv(4 * j + d, "poO%d" % (4 * j + d), F32, [C, 512])
                for k in range(16):
                    S.mm(po[:], yT.k(k)[:, k, c * C:(c + 1) * C], Wo.k((d, k // 4))[:, k, d * 512:(d + 1) * 512],
                         start=(k == 0), stop=(k == 15))
                S.tt("dve", res.k(d)[:, d * 512:(d + 1) * 512], po[:], xo[:, d * 512:(d + 1) * 512], ALU.add)
            S.memset("pool", sso[:, j:j + 1], 0.0)
            S.add("act", lambda e, j=j: e.activation(ob.h[:, :], res.h[:, :], AF.Square, accum_out=sso.h[:, j:j + 1]),
                  [res.k(d)[:] for d in range(4)] + [sso[:, j:j + 1]], [ob[:], sso[:, j:j + 1]])
            S.ts("dve", rso[:, j:j + 1], sso[:, j:j + 1], 1.0 / DM, ALU.mult, 1e-6, ALU.add)
            S.tt("pool", rso[:, j:j + 1], rso[:, j:j + 1], neghalf[:, 0:1], ALU.pow)
            S.add("dve", lambda e, j=j: e.scalar_tensor_tensor(ob.h[:, :], res.h[:, :], rso.h[:, j:j + 1], fnwb.h[:, :],
                                                              ALU.mult, ALU.mult),
                  [res.k(d)[:] for d in range(4)] + [rso[:, j:j + 1], fnwb[:]], [ob[:]])
            outs.append(S.dma("sp", D(outd[c * C:(c + 1) * C, :]), ob[:], "out"))
        S.emit(outs)
        build_program.stats = (S.max_counts, S.nwaits, {e: len(S.ops[e]) for e in ENGS})
    return nc


def _rot_tables(pos, half, theta):
    inv = (np.float32(theta) ** (-np.arange(half, dtype=np.float32) / np.float32(half))).astype(np.float32)
    ang = pos.astype(np.float32)[:, None] * inv[None, :]
    cos = np.cos(ang).astype(np.float32)
    sin = np.sin(ang).astype(np.float32)
    return cos, sin


def _weight_blocks(w_in, w_out):
    W = w_in[0]
    RW = 1024
    rq, rk, rv, rg = W[:, 0:RW], W[:, RW:2 * RW], W[:, 2 * RW:3 * RW], W[:, 3 * RW:4 * RW]
    aq = W[:, 4096:5120]
    ak = W[:, 5120:5376]
    av = W[:, 5376:5632]
    ag = W[:, 5632:6656]
    hs = lambda m, h: m[:, h * 128:(h + 1) * 128]
    blks = []
    for b in range(4):
        h = 2 * b
        blks.append(np.concatenate([hs(rk, h), hs(rv, h), hs(rk, h + 1), hs(rv, h + 1)], axis=1))
    for h in range(8):
        blks.append(np.concatenate([hs(rq, h), hs(rk, h), hs(rv, h), hs(rg, h)], axis=1))
    blks.append(np.concatenate([ak, av], axis=1))
    for g in range(4):
        h = 2 * g
        blks.append(np.concatenate([hs(aq, h), hs(aq, h + 1), hs(ag, h), hs(ag, h + 1)], axis=1))
    for d in range(4):
        blks.append(w_out[0][:, d * 512:(d + 1) * 512])
    return np.ascontiguousarray(np.stack(blks, axis=0), dtype=np.float32)


_NC_CACHE = {}
_DEBUG = {}


def kernel(x, w_in, w_out, norm_w, decay_fwd, decay_bwd, gn_w, gn_b, sink, final_norm_w):
    x = np.asarray(x, dtype=np.float32)
    wblk = _weight_blocks(np.asarray(w_in, np.float32), np.asarray(w_out, np.float32))
    nwT = np.ascontiguousarray(np.asarray(norm_w, np.float32)[0].reshape(16, 128).T)
    idx = np.arange(C, dtype=np.float32)
    expo = np.zeros((C, 20), np.float32)
    expo[:, 0] = idx + 1
    expo[:, 1] = -(idx + 1)
    expo[:, 2] = C - idx
    expo[:, 3] = -(C - idx)
    for c in range(16):
        expo[:, 4 + c] = 2047 - (128 * c + idx)
    jj = np.arange(C)[:, None]
    ii = np.arange(C)[None, :]
    kk = np.arange(384)[None, :]
    amask = np.where((kk >= jj) & (kk <= jj + 256), 0.0, -1e30).astype(np.float32)
    identf = np.eye(C, dtype=np.float32)
    in_maps = []
    for core in range(8):
        b, half = core // 2, core % 2
        if half == 1:
            xl = x[b]
            pos = np.arange(SEQ)
            d1, d2 = decay_fwd[0], decay_bwd[0]
            mf = (jj <= ii)
            mb = (jj > ii)
        else:
            xl = x[b][::-1]
            pos = SEQ - 1 - np.arange(SEQ)
            d1, d2 = decay_bwd[0], decay_fwd[0]
            mf = (jj < ii)
            mb = (jj >= ii)
        cosR, sinR = _rot_tables(pos, 64, 10000.0)
        rotR = np.concatenate([cosR, cosR, -sinR, sinR], axis=1).reshape(32, C, 256)
        posA = pos[HALF - C:]
        cosA, sinA = _rot_tables(posA, 16, 500000.0)
        ta = np.concatenate([cosA, cosA, -sinA, sinA], axis=1)
        rotA = np.concatenate([ta, ta * np.float32(QS)], axis=1).reshape(17, C, 128)
        msk = np.concatenate([mf.astype(np.float32), mb.astype(np.float32)], axis=1)
        in_maps.append({
            "xl": np.ascontiguousarray(xl, dtype=np.float32),
            "wblk": wblk,
            "rotR": np.ascontiguousarray(rotR, dtype=np.float32),
            "rotA": np.ascontiguousarray(rotA, dtype=np.float32),
            "msk": np.ascontiguousarray(msk),
            "amask": amask,
            "dec": np.concatenate([np.asarray(d1, np.float32), np.asarray(d2, np.float32)])[None, :].copy(),
            "nw": nwT,
            "fnw": np.asarray(final_norm_w, np.float32)[None, :].copy(),
            "gnw": np.asarray(gn_w, np.float32).reshape(1, 1024).copy(),
            "gnb": np.asarray(gn_b, np.float32).reshape(1, 1024).copy(),
            "sink": np.asarray(sink, np.float32).reshape(1, 8).copy(),
            "expo": expo,
            "identf": identf,
        })
    if _DEBUG.get("stop", 99) < 99:
        nc = build_program(stop=_DEBUG["stop"], n0=_DEBUG.get("n0", 16))
        cs = _DEBUG.get("cores", [0, 1])
        res = run_bass_kernel_spmd(nc, [in_maps[c] for c in cs], core_ids=list(range(len(cs))))
        return [np.asarray(res.results[i]["dbg"]) for i in range(len(cs))]
    if "nc" not in _NC_CACHE:
        _NC_CACHE["nc"] = build_program()
    nc = _NC_CACHE["nc"]
    res = run_bass_kernel_spmd(nc, in_maps, core_ids=list(range(8)))
    out = np.empty((4, SEQ, DM), np.float32)
    for core in range(8):
        b, half = core // 2, core % 2
        o = np.asarray(res.results[core]["out"], dtype=np.float32)
        if half == 1:
            out[b, HALF:] = o
        else:
            out[b, :HALF] = o[::-1]
    return out
```
